# Optimizing a Trainium2 kernel written in Bass

```python
import math
import jax, jax.numpy as jnp
from jax import lax
import numpy as np

D_MODEL = 2048
BATCH = 2
SEQ = 16384
DEPTH = 4
DEC_BATCH = 8
DEC_SEQ = 32
PAST_LEN = 2048

CHUNK = 64
N_HEADS = 8
HEAD_DIM = D_MODEL // N_HEADS // 2
D_FF = 5632
CONV_WIDTH = 3
Q_BLOCK = 128
N_ATTN = (DEPTH + 1) // 2
N_CONV = DEPTH // 2
EPS = 1e-5
SCALE = HEAD_DIM ** -0.5

kernel_name = "chunk_streaming_diffattn_shortconv_macaron"


def _rms(x, g):
    xf = x.astype(jnp.float32)
    y = xf * lax.rsqrt(jnp.mean(xf * xf, axis=-1, keepdims=True) + EPS)
    return (y * g.astype(jnp.float32)).astype(x.dtype)


def _swiglu(h, w_gu, w_down):
    g, u = jnp.split(h @ w_gu, 2, axis=-1)
    return (jax.nn.silu(g) * u) @ w_down


def _diff_lambda(lq, lk, lam_init):
    lq = lq.astype(jnp.float32)
    lk = lk.astype(jnp.float32)
    return jnp.exp(jnp.sum(lq[0] * lk[0])) - jnp.exp(jnp.sum(lq[1] * lk[1])) + lam_init


def _qkv(h, w_qkv):
    b, t, _ = h.shape
    q, k, v = jnp.split(h @ w_qkv, 3, axis=-1)
    q = q.reshape(b, t, 2 * N_HEADS, HEAD_DIM)
    k = k.reshape(b, t, 2 * N_HEADS, HEAD_DIM)
    v = v.reshape(b, t, N_HEADS, 2 * HEAD_DIM)
    return q, k, v


def _diff_core(q, k, v, lam, subln_g, lam_init, mask):
    s = jnp.einsum('bqhd,bkhd->bhqk', q.astype(jnp.float32), k.astype(jnp.float32)) * SCALE
    if mask is not None:
        s = jnp.where(mask[None, None], s, -jnp.inf)
    p = jax.nn.softmax(s, axis=-1)
    b, _, nq, nk = p.shape
    p = p.reshape(b, N_HEADS, 2, nq, nk)
    a = p[:, :, 0] - lam * p[:, :, 1]
    o = jnp.einsum('bhqk,bkhe->bqhe', a, v.astype(jnp.float32))
    o = _rms(o, subln_g) * (1.0 - lam_init)
    return o.astype(v.dtype)


def _attn_prompt(h, w_qkv, w_o, lam, subln_g, lam_init):
    b, s, _ = h.shape
    q, k, v = _qkv(h, w_qkv)
    nqb = s // Q_BLOCK
    qb = jnp.moveaxis(q.reshape(b, nqb, Q_BLOCK, 2 * N_HEADS, HEAD_DIM), 1, 0)
    k_chunk = jnp.arange(s) // CHUNK

    def block(args):
        q_i, b_i = args
        q_chunk = (b_i * Q_BLOCK + jnp.arange(Q_BLOCK)) // CHUNK
        mask = k_chunk[None, :] <= q_chunk[:, None]
        return _diff_core(q_i, k, v, lam, subln_g, lam_init, mask)

    o = lax.map(block, (qb, jnp.arange(nqb)))
    o = jnp.moveaxis(o, 0, 1).reshape(b, s, D_MODEL)
    return o @ w_o, k, v


def _attn_sample(h, cache_k, cache_v, w_qkv, w_o, lam, subln_g, lam_init):
    b, t, _ = h.shape
    q, k, v = _qkv(h, w_qkv)
    k_all = jnp.concatenate([cache_k.astype(k.dtype), k], axis=1)
    v_all = jnp.concatenate([cache_v.astype(v.dtype), v], axis=1)
    o = _diff_core(q, k_all, v_all, lam, subln_g, lam_init, None)
    return o.reshape(b, t, D_MODEL) @ w_o, k, v


def _short_conv(h, prev, w_in, w_conv, w_out):
    b_gate, c_gate, xt = jnp.split(h @ w_in, 3, axis=-1)
    g = c_gate * xt
    if prev is None:
        prev = jnp.zeros((h.shape[0], CONV_WIDTH - 1, D_MODEL), g.dtype)
    padded = jnp.concatenate([prev.astype(g.dtype), g], axis=1)
    y = lax.conv_general_dilated(
        padded, w_conv[:, None, :].astype(g.dtype), window_strides=(1,), padding='VALID',
        dimension_numbers=('NWC', 'WIO', 'NWC'), feature_group_count=D_MODEL)
    return (b_gate * y) @ w_out, padded[:, -(CONV_WIDTH - 1):]


def setup_inputs(seed: int = 0) -> dict:
    key = jax.random.key(seed)
    ks = jax.random.split(key, 20)
    f32 = jnp.float32
    nrm = lambda k, shape, scale: jax.random.normal(k, shape, f32) * scale
    return {
        "x_prompt": nrm(ks[0], (BATCH, SEQ, D_MODEL), 1.0),
        "x_sample": nrm(ks[1], (DEC_BATCH, DEC_SEQ, D_MODEL), 1.0),
        "cache_k_l0": nrm(ks[2], (DEC_BATCH, PAST_LEN, 2 * N_HEADS, HEAD_DIM), 1.0),
        "cache_v_l0": nrm(ks[3], (DEC_BATCH, PAST_LEN, N_HEADS, 2 * HEAD_DIM), 1.0),
        "state_conv_l1": nrm(ks[4], (DEC_BATCH, CONV_WIDTH - 1, D_MODEL), 1.0),
        "cache_k_l2": nrm(ks[5], (DEC_BATCH, PAST_LEN, 2 * N_HEADS, HEAD_DIM), 1.0),
        "cache_v_l2": nrm(ks[6], (DEC_BATCH, PAST_LEN, N_HEADS, 2 * HEAD_DIM), 1.0),
        "state_conv_l3": nrm(ks[7], (DEC_BATCH, CONV_WIDTH - 1, D_MODEL), 1.0),
        "norm_g": 1.0 + nrm(ks[8], (DEPTH, 3, D_MODEL), 0.01),
        "final_norm_g": 1.0 + nrm(ks[9], (D_MODEL,), 0.01),
        "ffn_w_gu": nrm(ks[10], (DEPTH, 2, D_MODEL, 2 * D_FF), D_MODEL ** -0.5),
        "ffn_w_down": nrm(ks[11], (DEPTH, 2, D_FF, D_MODEL), D_FF ** -0.5),
        "attn_w_qkv": nrm(ks[12], (N_ATTN, D_MODEL, 3 * D_MODEL), D_MODEL ** -0.5),
        "attn_w_o": nrm(ks[13], (N_ATTN, D_MODEL, D_MODEL), D_MODEL ** -0.5),
        "attn_lambda_q": nrm(ks[14], (N_ATTN, 2, HEAD_DIM), 0.1),
        "attn_lambda_k": nrm(ks[15], (N_ATTN, 2, HEAD_DIM), 0.1),
        "attn_subln_g": 1.0 + nrm(ks[16], (N_ATTN, 2 * HEAD_DIM), 0.01),
        "conv_w_in": nrm(ks[17], (N_CONV, D_MODEL, 3 * D_MODEL), D_MODEL ** -0.5),
        "conv_w": nrm(ks[18], (N_CONV, CONV_WIDTH, D_MODEL), CONV_WIDTH ** -0.5),
        "conv_w_out": nrm(ks[19], (N_CONV, D_MODEL, D_MODEL), D_MODEL ** -0.5),
    }


def reference(x_prompt, x_sample, cache_k_l0, cache_v_l0, state_conv_l1, cache_k_l2, cache_v_l2,
              state_conv_l3, norm_g, final_norm_g, ffn_w_gu, ffn_w_down, attn_w_qkv, attn_w_o,
              attn_lambda_q, attn_lambda_k, attn_subln_g, conv_w_in, conv_w, conv_w_out):
    attn_caches = ((cache_k_l0, cache_v_l0), (cache_k_l2, cache_v_l2))
    conv_states = (state_conv_l1, state_conv_l3)
    xp, xs = x_prompt, x_sample
    new_p, new_s = [], []
    for i in range(DEPTH):
        xp = xp + 0.5 * _swiglu(_rms(xp, norm_g[i, 0]), ffn_w_gu[i, 0], ffn_w_down[i, 0])
        xs = xs + 0.5 * _swiglu(_rms(xs, norm_g[i, 0]), ffn_w_gu[i, 0], ffn_w_down[i, 0])
        hp = _rms(xp, norm_g[i, 1])
        hs = _rms(xs, norm_g[i, 1])
        j = i // 2
        if i % 2 == 0:
            lam_init = 0.8 - 0.6 * math.exp(-0.3 * i)
            lam = _diff_lambda(attn_lambda_q[j], attn_lambda_k[j], lam_init)
            mp, kp, vp = _attn_prompt(hp, attn_w_qkv[j], attn_w_o[j], lam, attn_subln_g[j], lam_init)
            ck, cv = attn_caches[j]
            ms, k_s, v_s = _attn_sample(hs, ck, cv, attn_w_qkv[j], attn_w_o[j], lam,
                                        attn_subln_g[j], lam_init)
            new_p += [kp, vp]
            new_s += [k_s, v_s]
        else:
            mp, sp = _short_conv(hp, None, conv_w_in[j], conv_w[j], conv_w_out[j])
            ms, ss = _short_conv(hs, conv_states[j], conv_w_in[j], conv_w[j], conv_w_out[j])
            new_p += [sp]
            new_s += [ss]
        xp = xp + mp
        xs = xs + ms
        xp = xp + 0.5 * _swiglu(_rms(xp, norm_g[i, 2]), ffn_w_gu[i, 1], ffn_w_down[i, 1])
        xs = xs + 0.5 * _swiglu(_rms(xs, norm_g[i, 2]), ffn_w_gu[i, 1], ffn_w_down[i, 1])
    y_prompt = _rms(xp, final_norm_g)
    y_sample = _rms(xs, final_norm_g)
    k_l0_p, v_l0_p, c_l1_p, k_l2_p, v_l2_p, c_l3_p = new_p
    k_l0_s, v_l0_s, c_l1_s, k_l2_s, v_l2_s, c_l3_s = new_s
    return (y_prompt, y_sample, k_l0_p, v_l0_p, c_l1_p, k_l2_p, v_l2_p, c_l3_p,
            k_l0_s, v_l0_s, c_l1_s, k_l2_s, v_l2_s, c_l3_s)
```

```python
import math
from bisect import bisect_left
from contextlib import ExitStack

import numpy as np
import concourse.bass as bass
import concourse.mybir as mybir
from concourse.bass_utils import run_bass_kernel_spmd

F32 = mybir.dt.float32
BF16 = mybir.dt.bfloat16
AF = mybir.ActivationFunctionType
ALU = mybir.AluOpType

D = 2048
KC = 16
DFF = 5632
FC = 44
NGU = 22
NT = 8
TS = 512
NSMP = 32
NPR = NT * TS
TOK = NPR + NSMP
NTILES = NT + 1
EPS = 1e-5
SCALE = 128 ** -0.5
PAST = 2048
PKT = PAST // 128
NH2 = 2 * NT
STOP_AFTER = None


def _set_cfg(nt=8, dff=5632, past=2048):
    global DFF, FC, NGU, NT, NPR, TOK, NTILES, PAST, PKT, NH2
    DFF, FC, NGU = dff, dff // 128, dff // 256
    NT, NPR, TOK, NTILES = nt, nt * TS, nt * TS + NSMP, nt + 1
    PAST, PKT, NH2 = past, past // 128, 2 * nt


def tile_cols(ti):
    return (ti * TS, TS) if ti < NT else (NPR, NSMP)


class Tok:
    __slots__ = ("w", "r")

    def __init__(self):
        self.w = []
        self.r = []


class Eng:
    def __init__(self, name, h, sem, self_sync):
        self.name = name
        self.h = h
        self.sem = sem
        self.self_sync = self_sync
        self.count = 0
        self.pos = 0
        self.last = None
        self.sig_pos = []
        self.sig_tick = []
        self.seenE = {}
        self.seenD = {}


class Pool:
    def __init__(self, sems, base):
        self.sems = sems
        self.base = base
        self.vals = [0] * len(sems)
        self.nxt = 0


def _add_tag(lst, tag):
    if tag[0] == "E":
        for i, t in enumerate(lst):
            if t[0] == "E" and t[1] is tag[1]:
                if t[2] < tag[2]:
                    lst[i] = tag
                return
    lst.append(tag)


class Ctx:
    def __init__(self, nc):
        self.nc = nc
        self.PE = Eng("pe", nc.tensor, nc.alloc_semaphore("s_pe"), False)
        self.ACT = Eng("act", nc.scalar, nc.alloc_semaphore("s_act"), True)
        self.DVE = Eng("dve", nc.vector, nc.alloc_semaphore("s_dve"), True)
        self.POOL = Eng("pool", nc.gpsimd, nc.alloc_semaphore("s_pool"), True)
        self.SP = Eng("sp", nc.sync, None, False)
        self.engs = [self.PE, self.ACT, self.DVE, self.POOL, self.SP]
        self.dsems = []
        self.pools = {}
        for eng, n in ((self.SP, 32), (self.ACT, 24), (self.POOL, 24)):
            base = len(self.dsems)
            sems = [nc.alloc_semaphore("d_%s_%d" % (eng.name, i)) for i in range(n)]
            self.dsems += sems
            self.pools[eng.name] = Pool(sems, base)
        self.ccn = 0

    def resolve(self, B, pos):
        i = bisect_left(B.sig_pos, pos)
        if i < len(B.sig_pos):
            return B.sig_tick[i]
        B.count += 1
        B.last.then_inc(B.sem, 1)
        B.sig_pos.append(B.pos)
        B.sig_tick.append(B.count)
        return B.count

    def _wait(self, eng, deps):
        for d in deps:
            if d[0] == "E":
                B, pos = d[1], d[2]
                if B is eng and not eng.self_sync:
                    continue
                tick = self.resolve(B, pos)
                if eng.seenE.get(B.name, 0) < tick:
                    eng.h.wait_ge(B.sem, tick)
                    eng.seenE[B.name] = tick
            else:
                si, val = d[1], d[2]
                if eng.seenD.get(si, 0) < val:
                    eng.h.wait_ge(self.dsems[si], val)
                    eng.seenD[si] = val

    def _deps(self, reads, writes, adds):
        deps = []
        for t in reads:
            deps += t.w
        for t in writes:
            deps += t.w
            deps += t.r
        for t in adds:
            deps += t.r
        return deps

    def _record(self, tag, reads, writes, adds):
        for t in reads:
            _add_tag(t.r, tag)
        for t in writes:
            t.w = [tag]
            t.r = []
        for t in adds:
            _add_tag(t.w, tag)

    def op(self, eng, fn, reads=(), writes=(), adds=()):
        self._wait(eng, self._deps(reads, writes, adds))
        ins = fn()
        eng.pos += 1
        eng.last = ins
        self._record(("E", eng, eng.pos), reads, writes, adds)
        return ins

    def dma(self, q, out, in_, reads=(), writes=(), adds=()):
        self._wait(q, self._deps(reads, writes, adds))
        pool = self.pools[q.name]
        k = pool.nxt
        pool.nxt = (k + 1) % len(pool.sems)
        si = pool.base + k
        prev = pool.vals[k]
        if prev > 0 and q.seenD.get(si, 0) < prev:
            q.h.wait_ge(pool.sems[k], prev)
            q.seenD[si] = prev
        pool.vals[k] = prev + 16
        q.h.dma_start(out=out, in_=in_).then_inc(pool.sems[k], 16)
        self._record(("D", si, prev + 16), reads, writes, adds)

    def collective(self, in_ap, out_ap, groups, reads=(), writes=()):
        q = self.POOL
        self._wait(q, self._deps(reads, writes, ()))
        if self.ccn == 0:
            self.cc_si = len(self.dsems)
            self.dsems.append(self.nc.alloc_semaphore("cc_sem"))
        si = self.cc_si
        sem = self.dsems[si]
        if self.ccn > 0 and q.seenD.get(si, 0) < self.ccn:
            q.h.wait_ge(sem, self.ccn)
            q.seenD[si] = self.ccn
        self.ccn += 1
        q.h.collective_compute("AllGather", ALU.bypass, replica_groups=groups,
                               ins=[in_ap], outs=[out_ap]).then_inc(sem, 1)
        self._record(("D", si, self.ccn), reads, writes, ())

    def all_tags(self):
        tags = []
        for e in self.engs:
            if e.sem is not None and e.pos > 0:
                tags.append(("E", e, e.pos))
        for p in self.pools.values():
            for k, v in enumerate(p.vals):
                if v > 0:
                    tags.append(("D", p.base + k, v))
        if self.ccn > 0:
            tags.append(("D", self.cc_si, self.ccn))
        return tags

    def barrier(self):
        tags = self.all_tags()
        for e in self.engs:
            self._wait(e, [t for t in tags if not (t[0] == "E" and t[1] is e)])


class Buf:
    def __init__(self, ap, nchunk=0):
        self.ap = ap
        self.tok = Tok()
        self.ctoks = [Tok() for _ in range(nchunk)] if nchunk else None

    def ct(self, lo, hi=None):
        if self.ctoks is None:
            return [self.tok]
        return self.ctoks[lo:(lo + 1 if hi is None else hi)]

    def call(self):
        return [self.tok] if self.ctoks is None else list(self.ctoks)


def build_program():
    nc = bass.Bass("TRN2", target_bir_lowering=False)
    K = Ctx(nc)
    PE, ACT, DVE, POOL, SP = K.PE, K.ACT, K.DVE, K.POOL, K.SP

    def din(name, shape, dt=F32):
        return nc.dram_tensor(name, shape, dt, kind="ExternalInput").ap()

    def dout(name, shape, dt=F32):
        return nc.dram_tensor(name, shape, dt, kind="ExternalOutput").ap()

    def dscr(name, shape, dt):
        return nc.dram_tensor(name, shape, dt).ap()

    xin = din("xin", [D, TOK])
    ng_d = din("ng", [128, 13 * 16])
    wgu_d = din("wgu", [8 * NGU * 128, 8192])
    wd_d = din("wd", [8 * 16 * 128, DFF])
    wqkv_d = din("wqkv", [2 * 12 * 128, 8192])
    wo_d = din("wo", [2 * 4 * 128, 8192])
    win_d = din("win", [2 * 12 * 128, 8192])
    wout_d = din("wout", [2 * 4 * 128, 8192])
    lam_d = din("lam", [128, 8])
    subg_d = din("subg", [128, 4])
    convw_d = din("convw", [128, 2 * 3 * 16])
    cachek_d = din("cachek", [2 * 16 * 128, PAST])
    cachev_d = din("cachev", [2 * PAST, D])
    cstate_d = din("cstate", [2 * D, 2])
    mask_d = din("mask", [128, 16 * 512])
    sel_d = din("sel", [128, 8])

    yT_o = dout("yT", [D, TOK])
    kvo = dout("kvo", [4 * TOK, D])
    kv_o = {("k", 0): kvo[0:TOK, :], ("v", 0): kvo[TOK:2 * TOK, :],
            ("k", 1): kvo[2 * TOK:3 * TOK, :], ("v", 1): kvo[3 * TOK:4 * TOK, :]}
    cvo = dout("cvo", [4 * D, 2])
    cp_o = [cvo[0:D, :], cvo[D:2 * D, :]]
    cs_o = [cvo[2 * D:3 * D, :], cvo[3 * D:4 * D, :]]
    out_tok = Tok()

    xs_d = dscr("xs", [D, TOK], F32)
    xs_tok = [Tok() for _ in range(NTILES)]
    wgu_s = [dscr("wgu_s%d" % f, [NGU * 128, 8192], BF16) for f in range(8)]
    wd_s = [dscr("wd_s%d" % f, [16 * 128, DFF], BF16) for f in range(8)]
    wqkv_s = dscr("wqkv_s", [2 * 12 * 128, 8192], BF16)
    wo_s = dscr("wo_s", [2 * 4 * 128, 8192], BF16)
    win_s = dscr("win_s", [2 * 12 * 128, 8192], BF16)
    wout_s = dscr("wout_s", [2 * 4 * 128, 8192], BF16)
    qs_d = dscr("qs", [D, TOK], BF16)
    qs_tok = Tok()
    os_d = dscr("os", [D, TOK], BF16)
    os_tok = [Tok() for _ in range(NTILES)]
    kT_loc = dscr("kT_loc", [D, NPR], BF16)
    v_loc = dscr("v_loc", [NPR, D], BF16)
    kT_all = dscr("kT_all", [4 * D, NPR], BF16)
    v_all = dscr("v_all", [4 * NPR, D], BF16)
    NVP = NPR // 256
    kall_tok = [Tok() for _ in range(16)]
    vall_tok = [Tok() for _ in range(NVP)]
    ks_s = dscr("ks_s", [D, NSMP], BF16)
    vs_s = dscr("vs_s", [NSMP, D], BF16)
    kvloc_tok = Tok()
    kvall_tok = Tok()
    kvs_tok = Tok()
    tails_loc = dscr("tails_loc", [D, NH2], F32)
    tails_all = dscr("tails_all", [4 * D, NH2], F32)
    tl_tok = Tok()
    ta_tok = Tok()
    GROUPS = [[0, 1, 2, 3], [4, 5, 6, 7]]

    class WB:
        def __init__(self, src, dst):
            self.src = src
            self.dst = dst
            self.tok = Tok()
            self.done = False

    def wblocks(src, dst, first, n, dfirst=None):
        dfirst = first if dfirst is None else dfirst
        return [WB(src[(first + b) * 128:(first + b + 1) * 128, :],
                   dst[(dfirst + b) * 128:(dfirst + b + 1) * 128, :]) for b in range(n)]

    W_gu = [wblocks(wgu_d, wgu_s[f], f * NGU, NGU, 0) for f in range(8)]
    W_d = [wblocks(wd_d, wd_s[f], f * 16, 16, 0) for f in range(8)]
    W_qkv = [wblocks(wqkv_d, wqkv_s, j * 12, 12) for j in range(2)]
    W_o = [wblocks(wo_d, wo_s, j * 4, 4) for j in range(2)]
    W_in = [wblocks(win_d, win_s, j * 12, 12) for j in range(2)]
    W_out = [wblocks(wout_d, wout_s, j * 4, 4) for j in range(2)]

    def convert(blocks):
        for b in blocks:
            if not b.done:
                K.dma(POOL, b.dst, b.src, writes=[b.tok])
                b.done = True

    worder = []
    for l in range(4):
        worder.append(W_gu[2 * l] + W_d[2 * l])
        if l % 2 == 0:
            worder.append(W_qkv[l // 2])
            worder.append(W_o[l // 2])
        else:
            worder.append(W_in[l // 2] + W_out[l // 2])
        worder.append(W_gu[2 * l + 1] + W_d[2 * l + 1])
    wptr = [0]

    def convert_ahead(n):
        while wptr[0] < min(n, len(worder)):
            convert(worder[wptr[0]])
            wptr[0] += 1

    es0 = ExitStack()

    uid = [0]

    def salloc(es, name, shape, dt):
        uid[0] += 1
        t = es.enter_context(nc.sbuf_tensor("%s_%d" % (name, uid[0]), shape, dt))
        return t

    ones_bf = salloc(es0, "ones_bf", [128, 128], BF16)
    onesD = salloc(es0, "onesD", [128, 128], BF16)
    ones256 = salloc(es0, "ones256", [128, 128], BF16)
    ng = salloc(es0, "ng_sb", [128, 13 * 16], F32)
    lam_sb = salloc(es0, "lam_sb", [128, 8], F32)
    subg_sb = salloc(es0, "subg_sb", [128, 4], F32)
    convw_sb = salloc(es0, "convw_sb", [128, 96], F32)
    sel_sb = salloc(es0, "sel_sb", [128, 8], F32)
    neglam = salloc(es0, "neglam", [128, 2], F32)
    const_tok = Tok()
    psum = [Buf(es0.enter_context(nc.psum_tensor("ps%d" % i, [128, 512], F32))) for i in range(8)]

    eps_t = salloc(es0, "eps_t", [128, 1], F32)
    K.op(DVE, lambda: nc.vector.memset(eps_t[:], EPS), writes=[const_tok])
    K.op(DVE, lambda: nc.vector.memset(ones_bf[:], 1.0), writes=[const_tok])
    K.op(DVE, lambda: nc.vector.memset(onesD[:], 1.0 / D), writes=[const_tok])
    K.op(DVE, lambda: nc.vector.memset(ones256[:], 1.0 / 256), writes=[const_tok])
    K.dma(SP, ng[:], ng_d, adds=[const_tok])
    K.dma(SP, lam_sb[:], lam_d, adds=[const_tok])
    K.dma(SP, subg_sb[:], subg_d, adds=[const_tok])
    K.dma(SP, convw_sb[:], convw_d, adds=[const_tok])
    K.dma(SP, sel_sb[:], sel_d, adds=[const_tok])

    with ExitStack() as es:
        prod = salloc(es, "l_prod", [128, 4], F32)
        hi = salloc(es, "l_hi", [128, 4], BF16)
        hif = salloc(es, "l_hif", [128, 4], F32)
        lo = salloc(es, "l_lo", [128, 4], BF16)
        ex = salloc(es, "l_ex", [128, 4], F32)
        lt = Tok()
        lamv = lam_sb[:].rearrange("p (a b) -> p a b", b=2)
        K.op(DVE, lambda: nc.vector.tensor_tensor(prod[:], lamv[:, :, 0], lamv[:, :, 1], ALU.mult),
             reads=[const_tok], writes=[lt])
        K.op(DVE, lambda: nc.vector.tensor_copy(hi[:], prod[:]), reads=[lt], adds=[lt])
        K.op(DVE, lambda: nc.vector.tensor_copy(hif[:], hi[:]), reads=[lt], adds=[lt])
        K.op(DVE, lambda: nc.vector.tensor_tensor(hif[:], prod[:], hif[:], ALU.subtract), reads=[lt], adds=[lt])
        K.op(DVE, lambda: nc.vector.tensor_copy(lo[:], hif[:]), reads=[lt], adds=[lt])
        pb = psum[0]

        def _mm():
            nc.tensor.matmul(pb.ap[:, 0:4], ones_bf[:], hi[:], start=True, stop=False)
            return nc.tensor.matmul(pb.ap[:, 0:4], ones_bf[:], lo[:], start=False, stop=True)
        K.op(PE, _mm, reads=[lt, const_tok], writes=[pb.tok])
        K.op(ACT, lambda: nc.scalar.activation(out=ex[:], in_=pb.ap[:, 0:4], func=AF.Exp),
             reads=[pb.tok], writes=[lt])
        exv = ex[:].rearrange("p (j i) -> p j i", i=2)
        K.op(DVE, lambda: nc.vector.tensor_tensor(neglam[:], exv[:, :, 1], exv[:, :, 0], ALU.subtract),
             reads=[lt], writes=[const_tok])
        for j in range(2):
            li = 0.8 - 0.6 * math.exp(-0.3 * (2 * j))
            K.op(DVE, lambda j=j, li=li: nc.vector.tensor_scalar(neglam[:, j:j + 1], neglam[:, j:j + 1], -li, None, ALU.add),
                 reads=[const_tok], writes=[const_tok])
        K.barrier()

    xs_v = xs_d.rearrange("(c p) t -> p c t", p=128)
    xin_v = xin.rearrange("(c p) t -> p c t", p=128)
    yT_v = yT_o.rearrange("(c p) t -> p c t", p=128)

    class Stage:
        def __init__(self, es, nx=1, nw=3, with_sq=True):
            self.X = [Buf(salloc(es, "X%d" % i, [128, KC, TS], F32), KC) for i in range(nx)]
            self.H = Buf(salloc(es, "H", [128, KC, TS], BF16))
            self.SQ = [Buf(salloc(es, "SQ%d" % i, [128, 4, TS], BF16)) for i in range(2)]
            self.RSTD = Buf(salloc(es, "RSTD", [128, TS], F32))
            self.W = [Buf(salloc(es, "W%d" % i, [128, 8192], BF16)) for i in range(nw)]
            self.wi = 0
            self.pi = 0

        def wload(self, wb, width):
            s = self.W[self.wi % len(self.W)]
            self.wi += 1
            K.dma(SP, s.ap[:, :width], wb.dst, reads=[wb.tok], writes=[s.tok])
            return s

        def pbank(self, lo=0, hi=6):
            b = psum[lo + self.pi % (hi - lo)]
            self.pi += 1
            return b

    def load_x(xb, ti, src_v=None, src_tok=None):
        c0, n = tile_cols(ti)
        K.dma(SP, xb.ap[:, :, :n], (src_v if src_v is not None else xs_v)[:, :, c0:c0 + n],
              reads=[src_tok if src_tok is not None else xs_tok[ti]], writes=xb.call())

    def store_x(xb, ti):
        c0, n = tile_cols(ti)
        K.dma(ACT, xs_v[:, :, c0:c0 + n], xb.ap[:, :, :n], reads=xb.call(), writes=[xs_tok[ti]])

    def rstd_from(pb, out, n):
        K.op(ACT, lambda: nc.scalar.activation(out=out.ap[:, :n], in_=pb.ap[:, :n], func=AF.Sqrt, bias=eps_t[:], scale=1.0),
             reads=[pb.tok, const_tok], writes=[out.tok])
        K.op(DVE, lambda: nc.vector.reciprocal(out.ap[:, :n], out.ap[:, :n]), reads=[out.tok], writes=[out.tok])

    def rms_stats(st, xb, n, kc=KC, ones=None, chunk_ap=None):
        ones = onesD if ones is None else ones
        pb = psum[7]
        ng_ = (kc + 3) // 4
        for g in range(ng_):
            sq = st.SQ[g % 2]
            c_lo, c_hi = g * 4, min(kc, g * 4 + 4)
            src = xb.ap[:, c_lo:c_hi, :n] if chunk_ap is None else chunk_ap(c_lo, c_hi)
            K.op(ACT, lambda sq=sq, src=src, c_lo=c_lo, c_hi=c_hi: nc.scalar.activation(
                out=sq.ap[:, 0:c_hi - c_lo, :n], in_=src, func=AF.Square),
                reads=xb.ct(c_lo, c_hi), writes=[sq.tok])

            def _mm(sq=sq, c_lo=c_lo, c_hi=c_hi):
                ins = None
                for c in range(c_lo, c_hi):
                    ins = nc.tensor.matmul(pb.ap[:, :n], ones[:], sq.ap[:, c - c_lo, :n],
                                           start=(c == 0), stop=(c == kc - 1))
                return ins
            if g == 0:
                K.op(PE, _mm, reads=[sq.tok, const_tok], writes=[pb.tok])
            else:
                K.op(PE, _mm, reads=[sq.tok, const_tok], adds=[pb.tok])
        rstd_from(pb, st.RSTD, n)

    def rms_apply(st, xb, n, gidx, out_buf):
        for c in range(KC):
            K.op(DVE, lambda c=c: nc.vector.scalar_tensor_tensor(
                out_buf.ap[:, c, :n], xb.ap[:, c, :n], ng[:, gidx * 16 + c:gidx * 16 + c + 1],
                st.RSTD.ap[:, :n], ALU.mult, ALU.mult),
                reads=xb.ct(c) + [st.RSTD.tok, const_tok], adds=[out_buf.tok] if c else (),
                writes=() if c else [out_buf.tok])

    def mm_feat(st, wslot, sub, hbuf, n, pb, kc=KC, wstride=512, extra_reads=()):
        wv = wslot.ap[:, :kc * wstride].rearrange("p (k c) -> p k c", c=wstride)

        def _mm():
            ins = None
            for k in range(kc):
                ins = nc.tensor.matmul(pb.ap[:, :n], wv[:, k, sub * 128:(sub + 1) * 128], hbuf.ap[:, k, :n],
                                       start=(k == 0), stop=(k == kc - 1))
            return ins
        K.op(PE, _mm, reads=[wslot.tok, hbuf.tok] + list(extra_reads), writes=[pb.tok])

    def resid_add(st, xb, m, n, pb, scale):
        if scale == 1.0:
            K.op(DVE, lambda: nc.vector.tensor_tensor(xb.ap[:, m, :n], pb.ap[:, :n], xb.ap[:, m, :n], ALU.add),
                 reads=[pb.tok], writes=xb.ct(m))
        else:
            K.op(DVE, lambda: nc.vector.scalar_tensor_tensor(xb.ap[:, m, :n], pb.ap[:, :n], scale, xb.ap[:, m, :n],
                                                            ALU.mult, ALU.add),
                 reads=[pb.tok], writes=xb.ct(m))

    def stage_ffn(f, gidx, first=False):
        K.barrier()
        with ExitStack() as es:
            st = Stage(es, nx=2, nw=3)
            HID = Buf(salloc(es, "HID", [128, FC, TS], BF16))
            SG = [Buf(salloc(es, "SG%d" % i, [128, TS], BF16)) for i in range(2)]
            src_v = xin_v if first else None
            src_tok = Tok() if first else None

            def prep_load(ti):
                load_x(st.X[ti % 2], ti, src_v, src_tok)

            def prep_sq(ti):
                rms_stats(st, st.X[ti % 2], tile_cols(ti)[1])

            def prep_h(ti):
                rms_apply(st, st.X[ti % 2], tile_cols(ti)[1], gidx, st.H)

            prep_load(0)
            prep_sq(0)
            prep_h(0)
            sgi = 0
            for ti in range(NTILES):
                c0, n = tile_cols(ti)
                xb = st.X[ti % 2]
                if ti + 1 < NTILES:
                    prep_load(ti + 1)
                for j in range(NGU):
                    ws = st.wload(W_gu[f][j], 8192)
                    for sub in range(2):
                        pg = st.pbank(0, 4)
                        pu = st.pbank(0, 4)
                        mm_feat(st, ws, sub, st.H, n, pg)
                        mm_feat(st, ws, 2 + sub, st.H, n, pu)
                        sg = SG[sgi % 2]
                        sgi += 1
                        K.op(ACT, lambda sg=sg, pg=pg: nc.scalar.activation(out=sg.ap[:, :n], in_=pg.ap[:, :n], func=AF.Silu),
                             reads=[pg.tok], writes=[sg.tok])
                        hc = 2 * j + sub
                        K.op(DVE, lambda sg=sg, pu=pu, hc=hc: nc.vector.tensor_tensor(
                            HID.ap[:, hc, :n], sg.ap[:, :n], pu.ap[:, :n], ALU.mult),
                            reads=[sg.tok, pu.tok], adds=[HID.tok] if (j or sub) else (),
                            writes=() if (j or sub) else [HID.tok])
                for m in range(16):
                    if m == 4 and ti + 1 < NTILES:
                        prep_sq(ti + 1)
                        prep_h(ti + 1)
                    ws = st.wload(W_d[f][m], DFF)
                    pd = psum[4 + m % 2]
                    mm_feat(st, ws, 0, HID, n, pd, kc=FC, wstride=128)
                    resid_add(st, xb, m, n, pd, 0.5)
                store_x(xb, ti)

    def stage_qkv(j, gidx):
        K.barrier()
        with ExitStack() as es:
            st = Stage(es, nx=1, nw=3)
            QT = Buf(salloc(es, "QT", [128, 16, TS], BF16))
            KT = Buf(salloc(es, "KT", [128, 16, TS], BF16))
            TMF = [Buf(salloc(es, "TMF%d" % i, [128, 4, 512], F32)) for i in range(2)]
            TMB = [Buf(salloc(es, "TMB%d" % i, [128, 4, 512], BF16)) for i in range(2)]
            tmi = 0
            qs_v = qs_d.rearrange("(c p) t -> p c t", p=128)
            kl_v = kT_loc.rearrange("(c p) t -> p c t", p=128)
            kss_v = ks_s.rearrange("(c p) t -> p c t", p=128)
            for ti in range(NTILES):
                c0, n = tile_cols(ti)
                xb = st.X[0]
                load_x(xb, ti)
                rms_stats(st, xb, n)
                rms_apply(st, xb, n, gidx, st.H)
                ntb = (n + 127) // 128
                for blk in range(12):
                    ws = st.wload(W_qkv[j][blk], 8192)
                    wv = ws.ap[:].rearrange("p (k c) -> p k c", c=512)
                    if blk < 8:
                        dstb = QT if blk < 4 else KT
                        for sub in range(4):
                            pb = st.pbank(0, 7)
                            mm_feat(st, ws, sub, st.H, n, pb)
                            hm = (blk % 4) * 4 + sub
                            first_w = (blk % 4 == 0 and sub == 0)
                            K.op(ACT, lambda pb=pb, hm=hm, dstb=dstb: nc.scalar.copy(dstb.ap[:, hm, :n], pb.ap[:, :n]),
                                 reads=[pb.tok], adds=() if first_w else [dstb.tok],
                                 writes=[dstb.tok] if first_w else ())
                    if blk >= 4:
                        tf = TMF[tmi % 2]
                        tb_ = TMB[tmi % 2]
                        tmi += 1
                        for tb in range(ntb):
                            mtok = min(128, n - tb * 128)
                            pb = st.pbank(0, 7)

                            def _mm(pb=pb, tb=tb, mtok=mtok, wv=wv):
                                ins = None
                                for k in range(KC):
                                    ins = nc.tensor.matmul(pb.ap[:mtok, :], st.H.ap[:, k, tb * 128:tb * 128 + mtok],
                                                           wv[:, k, :], start=(k == 0), stop=(k == KC - 1))
                                return ins
                            K.op(PE, _mm, reads=[ws.tok, st.H.tok], writes=[pb.tok])
                            K.op(DVE, lambda pb=pb, tb=tb, mtok=mtok, tf=tf: nc.vector.tensor_copy(tf.ap[:mtok, tb, :], pb.ap[:mtok, :]),
                                 reads=[pb.tok], adds=[tf.tok] if tb else (), writes=() if tb else [tf.tok])
                            if blk >= 8:
                                K.op(ACT, lambda tf=tf, tb=tb, mtok=mtok, tb_=tb_: nc.scalar.copy(tb_.ap[:mtok, tb, :], tf.ap[:mtok, tb, :]),
                                     reads=[tf.tok], adds=[tb_.tok] if tb else (), writes=() if tb else [tb_.tok])
                        which = "k" if blk < 8 else "v"
                        cb = (blk % 4) * 512
                        mt0 = min(128, n)
                        o_v = kv_o[(which, j)][c0:c0 + n, cb:cb + 512].rearrange("(b p) c -> p b c", p=mt0)
                        K.dma(ACT, o_v, tf.ap[:mt0, :ntb, :], reads=[tf.tok], adds=[out_tok])
                        if blk >= 8:
                            if ti < NT:
                                d_v = v_loc[c0:c0 + n, cb:cb + 512].rearrange("(b p) c -> p b c", p=128)
                                K.dma(ACT, d_v, tb_.ap[:, :ntb, :], reads=[tb_.tok], adds=[kvloc_tok])
                            else:
                                K.dma(ACT, vs_s[:, cb:cb + 512], tb_.ap[:NSMP, 0, :], reads=[tb_.tok], adds=[kvs_tok])
                K.dma(ACT, qs_v[:, :, c0:c0 + n], QT.ap[:, :, :n], reads=[QT.tok], adds=[qs_tok])
                if ti < NT:
                    K.dma(ACT, kl_v[:, :, c0:c0 + n], KT.ap[:, :, :n], reads=[KT.tok], adds=[kvloc_tok])
                else:
                    K.dma(ACT, kss_v[:, :, :], KT.ap[:, :, :n], reads=[KT.tok], adds=[kvs_tok])
        for i in range(NVP):
            K.collective(v_loc[i * 256:(i + 1) * 256, :], v_all[i * 1024:(i + 1) * 1024, :], GROUPS,
                         reads=[kvloc_tok], writes=[vall_tok[i]])
        for hm in range(16):
            K.collective(kT_loc[hm * 128:(hm + 1) * 128, :], kT_all[hm * 512:(hm + 1) * 512, :], GROUPS,
                         reads=[kvloc_tok], writes=[kall_tok[hm]])

    def stage_attn(j):
        K.barrier()
        with ExitStack() as es:
            KTb = salloc(es, "A_KT", [128, 2, 4 * NPR], BF16)
            Vb = salloc(es, "A_V", [128, 4 * NPR // 128, 256], BF16)
            kv_tok = [Tok() for _ in range(NT)]
            QTb = Buf(salloc(es, "A_QT", [128, 2, TOK], BF16))
            MK = Buf(salloc(es, "A_MK", [128, 16, 512], BF16))
            CK = Buf(salloc(es, "A_CK", [128, 2, PAST + NSMP], BF16))
            CV = Buf(salloc(es, "A_CV", [128, PKT + 1, 256], BF16))
            PT = [Buf(salloc(es, "A_PT%d" % i, [128, 512], BF16)) for i in range(4)]
            RS = [Buf(salloc(es, "A_RS%d" % i, [128, 512], F32)) for i in range(2)]
            T1 = Buf(salloc(es, "A_T1", [128, 512], F32))
            T2 = Buf(salloc(es, "A_T2", [128, 512], F32))
            OH = [Buf(salloc(es, "A_OH%d" % i, [128, 512], F32)) for i in range(2)]
            SQ = [Buf(salloc(es, "A_SQ%d" % i, [128, 512], BF16)) for i in range(2)]
            RSTD = Buf(salloc(es, "A_RSTD", [128, 512], F32))
            OT = Buf(salloc(es, "A_OT", [128, 2, 512], BF16))
            gsc = Buf(salloc(es, "A_gsc", [128, 2], F32))
            li = 0.8 - 0.6 * math.exp(-0.3 * (2 * j))
            K.op(DVE, lambda: nc.vector.tensor_scalar(gsc.ap[:], subg_sb[:, 2 * j:2 * j + 2], 1.0 - li, None, ALU.mult),
                 reads=[const_tok], writes=[gsc.tok])
            K.dma(POOL, MK.ap[:].rearrange("p a q -> p (a q)"), mask_d, writes=[MK.tok])
            kall_v = kT_all.rearrange("(c r p) t -> p c r t", p=128, r=4)
            vall_v = v_all.rearrange("(i r u p) e -> p i r u e", p=128, r=4, u=2)
            qs_v = qs_d.rearrange("(c p) t -> p c t", p=128)
            os_v = os_d.rearrange("(c p) t -> p c t", p=128)
            ck_v = cachek_d.rearrange("(l c p) t -> p l c t", p=128, c=16)
            cv_v = cachev_d.rearrange("(l b p) e -> p l b e", p=128, b=PKT)
            kss_v = ks_s.rearrange("(c p) t -> p c t", p=128)
            O = [[psum[0], psum[1]], [psum[2], psum[3]]]
            S = [psum[4], psum[5]]
            P = [psum[6], psum[7]]
            cnt = [0]

            def attend(h, qv, n, keyfn, nkt, maskfn):
                pend = None
                for kt in range(nkt + 1):
                    cur = None
                    if kt < nkt:
                        klhs, vlhs, olhs, nk, toks = keyfn(kt)
                        cur = []
                        for m in range(2):
                            sc = P[cnt[0] % 2]
                            pt = PT[cnt[0] % 4]
                            cnt[0] += 1
                            K.op(PE, lambda sc=sc, m=m, klhs=klhs, nk=nk: nc.tensor.matmul(
                                sc.ap[:nk, :n], klhs(m), qv(m), start=True, stop=True),
                                reads=toks + [QTb.tok], writes=[sc.tok])
                            K.op(ACT, lambda sc=sc, pt=pt, nk=nk: nc.scalar.activation(
                                out=pt.ap[:nk, :n], in_=sc.ap[:nk, :n], func=AF.Exp, scale=SCALE),
                                reads=[sc.tok], writes=[pt.tok])
                            mk = maskfn(kt)
                            if mk is not None:
                                K.op(DVE, lambda pt=pt, mk=mk, nk=nk: nc.vector.tensor_tensor(
                                    pt.ap[:nk, :n], pt.ap[:nk, :n], mk, ALU.mult),
                                    reads=[MK.tok], writes=[pt.tok])
                            cur.append((pt, vlhs, olhs, nk, toks, kt))
                    if pend is not None:
                        for m, (pt, vlhs, olhs, nk, toks, pk) in enumerate(pend):
                            def _av(pt=pt, vlhs=vlhs, olhs=olhs, nk=nk, pk=pk, m=m):
                                st_, sp_ = (pk == 0), (pk == nkt - 1)
                                nc.tensor.matmul(O[m][0].ap[:, :n], vlhs(0), pt.ap[:nk, :n], start=st_, stop=sp_)
                                nc.tensor.matmul(O[m][1].ap[:, :n], vlhs(1), pt.ap[:nk, :n], start=st_, stop=sp_)
                                return nc.tensor.matmul(S[m].ap[:, :n], olhs, pt.ap[:nk, :n], start=st_, stop=sp_)
                            wr = [O[m][0].tok, O[m][1].tok, S[m].tok]
                            K.op(PE, _av, reads=[pt.tok, const_tok] + toks,
                                 writes=wr if pk == 0 else (), adds=() if pk == 0 else wr)
                    pend = cur

            def finish(h, n, c0):
                for m in range(2):
                    K.op(DVE, lambda m=m: nc.vector.reciprocal(RS[m].ap[:, :n], S[m].ap[:, :n]),
                         reads=[S[m].tok], writes=[RS[m].tok])
                for half in range(2):
                    K.op(DVE, lambda half=half: nc.vector.tensor_tensor(T1.ap[:, :n], O[0][half].ap[:, :n], RS[0].ap[:, :n], ALU.mult),
                         reads=[O[0][half].tok, RS[0].tok], writes=[T1.tok])
                    K.op(DVE, lambda half=half: nc.vector.tensor_tensor(T2.ap[:, :n], O[1][half].ap[:, :n], RS[1].ap[:, :n], ALU.mult),
                         reads=[O[1][half].tok, RS[1].tok], writes=[T2.tok])
                    K.op(DVE, lambda half=half: nc.vector.scalar_tensor_tensor(
                        OH[half].ap[:, :n], T2.ap[:, :n], neglam[:, j:j + 1], T1.ap[:, :n], ALU.mult, ALU.add),
                        reads=[T1.tok, T2.tok, const_tok], writes=[OH[half].tok])
                    K.op(ACT, lambda half=half: nc.scalar.activation(out=SQ[half].ap[:, :n], in_=OH[half].ap[:, :n], func=AF.Square),
                         reads=[OH[half].tok], writes=[SQ[half].tok])
                pb = P[0]

                def _mm():
                    nc.tensor.matmul(pb.ap[:, :n], ones256[:], SQ[0].ap[:, :n], start=True, stop=False)
                    return nc.tensor.matmul(pb.ap[:, :n], ones256[:], SQ[1].ap[:, :n], start=False, stop=True)
                K.op(PE, _mm, reads=[SQ[0].tok, SQ[1].tok, const_tok], writes=[pb.tok])
                rstd_from(pb, RSTD, n)
                for half in range(2):
                    K.op(DVE, lambda half=half: nc.vector.scalar_tensor_tensor(
                        OT.ap[:, half, :n], OH[half].ap[:, :n], gsc.ap[:, half:half + 1], RSTD.ap[:, :n], ALU.mult, ALU.mult),
                        reads=[OH[half].tok, RSTD.tok, gsc.tok], writes=[OT.tok] if half == 0 else (),
                        adds=() if half == 0 else [OT.tok])
                ti = c0 // TS
                K.dma(ACT, os_v[:, 2 * h:2 * h + 2, c0:c0 + n], OT.ap[:, :, :n], reads=[OT.tok], adds=[os_tok[ti]])

            for h in range(8):
                K.dma(SP, QTb.ap[:, :, :], qs_v[:, 2 * h:2 * h + 2, :], reads=[qs_tok], writes=[QTb.tok])
                for kr in range(NT):
                    for m in range(2):
                        K.dma(SP, KTb[:, m, kr * 2048:(kr + 1) * 2048].rearrange("p (r t) -> p r t", r=4),
                              kall_v[:, 2 * h + m, :, kr * 512:(kr + 1) * 512],
                              reads=[kall_tok[2 * h + m]], writes=[kv_tok[kr]] if m == 0 else (), adds=() if m == 0 else [kv_tok[kr]])
                    for r in range(4):
                        for u2 in range(2):
                            kt0 = (kr * 4 + r) * 4 + 2 * u2
                            K.dma(SP, Vb[:, kt0:kt0 + 2, :],
                                  vall_v[:, 2 * kr + u2, r, :, h * 256:(h + 1) * 256],
                                  reads=[vall_tok[2 * kr + u2]], adds=[kv_tok[kr]])
                for m in range(2):
                    K.dma(POOL, CK.ap[:, m, 0:PAST], ck_v[:, j, 2 * h + m, :], writes=[CK.tok] if m == 0 else (),
                          adds=() if m == 0 else [CK.tok])
                K.dma(SP, CK.ap[:, :, PAST:PAST + NSMP], kss_v[:, 2 * h:2 * h + 2, :], reads=[kvs_tok], adds=[CK.tok])
                K.dma(POOL, CV.ap[:, 0:PKT, :], cv_v[:, j, :, h * 256:(h + 1) * 256], writes=[CV.tok])
                K.dma(SP, CV.ap[:NSMP, PKT, :], vs_s[:, h * 256:(h + 1) * 256], reads=[kvs_tok], adds=[CV.tok])

                for i in range(NT):
                    def keyfn(kt, i=i):
                        kr = kt // 16
                        return (lambda m, kt=kt: KTb[:, m, kt * 128:(kt + 1) * 128],
                                lambda half, kt=kt: Vb[:, kt, half * 128:(half + 1) * 128],
                                ones_bf[:], 128, [kv_tok[kr]])

                    def maskfn(kt, i=i):
                        if kt // 16 < i:
                            return None
                        return MK.ap[:, kt % 16, :]
                    attend(h, lambda m, i=i: QTb.ap[:, m, i * TS:(i + 1) * TS], TS, keyfn, (i + 1) * 16, maskfn)
                    finish(h, TS, i * TS)

                def keyfn_s(kt):
                    if kt < PKT:
                        return (lambda m, kt=kt: CK.ap[:, m, kt * 128:(kt + 1) * 128],
                                lambda half, kt=kt: CV.ap[:, kt, half * 128:(half + 1) * 128],
                                ones_bf[:], 128, [CK.tok, CV.tok])
                    return (lambda m: CK.ap[:, m, PAST:PAST + NSMP],
                            lambda half: CV.ap[:NSMP, PKT, half * 128:(half + 1) * 128],
                            ones_bf[:NSMP, :], NSMP, [CK.tok, CV.tok])
                attend(h, lambda m: QTb.ap[:, m, NPR:NPR + NSMP], NSMP, keyfn_s, PKT + 1, lambda kt: None)
                finish(h, NSMP, NPR)

    def stage_proj(wb, act_d, act_tok):
        K.barrier()
        with ExitStack() as es:
            st = Stage(es, nx=1, nw=3)
            act_v = act_d.rearrange("(c p) t -> p c t", p=128)
            for ti in range(NTILES):
                c0, n = tile_cols(ti)
                xb = st.X[0]
                load_x(xb, ti)
                K.dma(SP, st.H.ap[:, :, :n], act_v[:, :, c0:c0 + n], reads=[act_tok[ti]], writes=[st.H.tok])
                for blk in range(4):
                    ws = st.wload(wb[blk], 8192)
                    for sub in range(4):
                        pb = st.pbank(0, 7)
                        mm_feat(st, ws, sub, st.H, n, pb)
                        resid_add(st, xb, blk * 4 + sub, n, pb, 1.0)
                store_x(xb, ti)

    def stage_conv(j, gidx):
        K.barrier()
        with ExitStack() as es:
            st = Stage(es, nx=1, nw=2)
            Bg = Buf(salloc(es, "C_B", [128, KC, TS], F32))
            G = Buf(salloc(es, "C_G", [128, KC, TS + 2], F32))
            U = Buf(salloc(es, "C_U", [128, KC, TS], BF16))
            CT = [Buf(salloc(es, "C_CT%d" % i, [128, TS], F32)) for i in range(2)]
            Y = [Buf(salloc(es, "C_Y%d" % i, [128, TS], F32)) for i in range(2)]
            XT = Buf(salloc(es, "C_XT", [128, KC, NH2], F32))
            HT = Buf(salloc(es, "C_HT", [128, KC, NH2], BF16))
            GT = Buf(salloc(es, "C_GT", [128, KC, NH2], F32))
            TA = Buf(salloc(es, "C_TA", [128, 4, KC, NH2], F32))
            HAL = Buf(salloc(es, "C_HAL", [128, KC, NH2], F32))
            CST = Buf(salloc(es, "C_CST", [128, KC, 2], F32))
            cw = convw_sb[:].rearrange("p (j k c) -> p j k c", j=2, k=3)
            for i in range(NT):
                K.dma(SP, XT.ap[:, :, 2 * i:2 * i + 2], xs_v[:, :, i * TS + TS - 2:i * TS + TS],
                      reads=[xs_tok[i]], writes=[XT.tok] if i == 0 else (), adds=() if i == 0 else [XT.tok])
            pbs = psum[7]
            SQt = Buf(salloc(es, "C_SQt", [128, KC, NH2], BF16))
            K.op(ACT, lambda: nc.scalar.activation(out=SQt.ap[:], in_=XT.ap[:], func=AF.Square),
                 reads=[XT.tok], writes=[SQt.tok])

            def _mm():
                ins = None
                for c in range(KC):
                    ins = nc.tensor.matmul(pbs.ap[:, :NH2], onesD[:], SQt.ap[:, c, :], start=(c == 0), stop=(c == KC - 1))
                return ins
            K.op(PE, _mm, reads=[SQt.tok, const_tok], writes=[pbs.tok])
            rstd_from(pbs, st.RSTD, NH2)
            rms_apply(st, XT, NH2, gidx, HT)
            for jj in range(8):
                ws = st.wload(W_in[j][4 + jj], 8192)
                for sub in range(2):
                    pc = st.pbank(0, 6)
                    px = st.pbank(0, 6)
                    mm_feat(st, ws, sub, HT, NH2, pc)
                    mm_feat(st, ws, 2 + sub, HT, NH2, px)
                    ct = CT[(2 * jj + sub) % 2]
                    K.op(ACT, lambda ct=ct, pc=pc: nc.scalar.copy(ct.ap[:, :NH2], pc.ap[:, :NH2]), reads=[pc.tok], writes=[ct.tok])
                    ch = 2 * jj + sub
                    K.op(DVE, lambda ct=ct, px=px, ch=ch: nc.vector.tensor_tensor(GT.ap[:, ch, :], ct.ap[:, :NH2], px.ap[:, :NH2], ALU.mult),
                         reads=[ct.tok, px.tok], adds=[GT.tok] if ch else (), writes=() if ch else [GT.tok])
            K.dma(ACT, tails_loc.rearrange("(c p) t -> p c t", p=128), GT.ap[:], reads=[GT.tok], writes=[tl_tok])
            K.collective(tails_loc, tails_all, GROUPS, reads=[tl_tok], writes=[ta_tok])
            ta_v = tails_all.rearrange("(r c p) t -> p r c t", p=128, c=KC)
            for r in range(4):
                K.dma(SP, TA.ap[:, r, :, :], ta_v[:, r, :, :], reads=[ta_tok], writes=[TA.tok] if r == 0 else (),
                      adds=() if r == 0 else [TA.tok])
            K.op(DVE, lambda: nc.vector.tensor_scalar(HAL.ap[:], TA.ap[:, 0, :, :], sel_sb[:, 0:1], None, ALU.mult),
                 reads=[TA.tok, const_tok], writes=[HAL.tok])
            for r in range(1, 4):
                K.op(DVE, lambda r=r: nc.vector.scalar_tensor_tensor(HAL.ap[:], TA.ap[:, r, :, :], sel_sb[:, r:r + 1], HAL.ap[:], ALU.mult, ALU.add),
                     reads=[TA.tok], writes=[HAL.tok])
            for r in range(4):
                K.op(DVE, lambda r=r: nc.vector.scalar_tensor_tensor(HAL.ap[:, :, 2:NH2], TA.ap[:, r, :, 0:NH2 - 2], sel_sb[:, 4 + r:5 + r],
                                                                HAL.ap[:, :, 2:NH2], ALU.mult, ALU.add),
                     reads=[TA.tok], writes=[HAL.tok])
            K.dma(SP, CST.ap[:], cstate_d.rearrange("(l c p) t -> p l c t", p=128, c=KC)[:, j, :, :], writes=[CST.tok])
            for ti in range(NTILES):
                c0, n = tile_cols(ti)
                xb = st.X[0]
                load_x(xb, ti)
                rms_stats(st, xb, n)
                rms_apply(st, xb, n, gidx, st.H)
                halo = HAL.ap[:, :, 2 * ti:2 * ti + 2] if ti < NT else CST.ap[:]
                K.op(DVE, lambda halo=halo: nc.vector.tensor_copy(G.ap[:, :, 0:2], halo),
                     reads=[HAL.tok, CST.tok], writes=[G.tok])
                for blk in range(4):
                    ws = st.wload(W_in[j][blk], 8192)
                    for sub in range(4):
                        pb = st.pbank(0, 6)
                        mm_feat(st, ws, sub, st.H, n, pb)
                        ch = blk * 4 + sub
                        K.op(ACT, lambda pb=pb, ch=ch: nc.scalar.copy(Bg.ap[:, ch, :n], pb.ap[:, :n]),
                             reads=[pb.tok], adds=[Bg.tok] if ch else (), writes=() if ch else [Bg.tok])
                for jj in range(8):
                    ws = st.wload(W_in[j][4 + jj], 8192)
                    for sub in range(2):
                        pc = st.pbank(0, 6)
                        px = st.pbank(0, 6)
                        mm_feat(st, ws, sub, st.H, n, pc)
                        mm_feat(st, ws, 2 + sub, st.H, n, px)
                        ct = CT[(2 * jj + sub) % 2]
                        K.op(ACT, lambda ct=ct, pc=pc: nc.scalar.copy(ct.ap[:, :n], pc.ap[:, :n]), reads=[pc.tok], writes=[ct.tok])
                        ch = 2 * jj + sub
                        K.op(DVE, lambda ct=ct, px=px, ch=ch: nc.vector.tensor_tensor(G.ap[:, ch, 2:2 + n], ct.ap[:, :n], px.ap[:, :n], ALU.mult),
                             reads=[ct.tok, px.tok], adds=[G.tok])
                        y = Y[ch % 2]
                        K.op(DVE, lambda y=y, ch=ch: nc.vector.tensor_scalar(y.ap[:, :n], G.ap[:, ch, 0:n], cw[:, j, 0, ch:ch + 1], None, ALU.mult),
                             reads=[G.tok, const_tok], writes=[y.tok])
                        K.op(DVE, lambda y=y, ch=ch: nc.vector.scalar_tensor_tensor(y.ap[:, :n], G.ap[:, ch, 1:n + 1], cw[:, j, 1, ch:ch + 1], y.ap[:, :n], ALU.mult, ALU.add),
                             reads=[G.tok], writes=[y.tok])
                        K.op(DVE, lambda y=y, ch=ch: nc.vector.scalar_tensor_tensor(y.ap[:, :n], G.ap[:, ch, 2:n + 2], cw[:, j, 2, ch:ch + 1], y.ap[:, :n], ALU.mult, ALU.add),
                             reads=[G.tok], writes=[y.tok])
                        K.op(DVE, lambda y=y, ch=ch: nc.vector.tensor_tensor(U.ap[:, ch, :n], y.ap[:, :n], Bg.ap[:, ch, :n], ALU.mult),
                             reads=[y.tok, Bg.tok], adds=[U.tok] if ch else (), writes=() if ch else [U.tok])
                if ti == NT - 1:
                    K.dma(ACT, cp_o[j].rearrange("(c p) t -> p c t", p=128), G.ap[:, :, n:n + 2], reads=[G.tok], adds=[out_tok])
                if ti == NT:
                    K.dma(ACT, cs_o[j].rearrange("(c p) t -> p c t", p=128), G.ap[:, :, n:n + 2], reads=[G.tok], adds=[out_tok])
                for blk in range(4):
                    ws = st.wload(W_out[j][blk], 8192)
                    for sub in range(4):
                        pb = st.pbank(0, 6)
                        mm_feat(st, ws, sub, U, n, pb)
                        resid_add(st, xb, blk * 4 + sub, n, pb, 1.0)
                store_x(xb, ti)

    def stage_final():
        K.barrier()
        with ExitStack() as es:
            st = Stage(es, nx=2, nw=1)
            Yb = [Buf(salloc(es, "F_Y%d" % i, [128, KC, TS], F32)) for i in range(2)]
            for ti in range(NTILES):
                c0, n = tile_cols(ti)
                xb = st.X[ti % 2]
                load_x(xb, ti)
                rms_stats(st, xb, n)
                rms_apply(st, xb, n, 12, Yb[ti % 2])
                K.dma(ACT, yT_v[:, :, c0:c0 + n], Yb[ti % 2].ap[:, :, :n], reads=[Yb[ti % 2].tok], adds=[out_tok])

    plan = []
    for l in range(4):
        plan.append((True, lambda l=l: stage_ffn(2 * l, l * 3 + 0, first=(l == 0))))
        if l % 2 == 0:
            plan.append((True, lambda l=l: stage_qkv(l // 2, l * 3 + 1)))
            plan.append((False, lambda l=l: stage_attn(l // 2)))
            plan.append((True, lambda l=l: stage_proj(W_o[l // 2], os_d, os_tok)))
        else:
            plan.append((True, lambda l=l: stage_conv(l // 2, l * 3 + 1)))
        plan.append((True, lambda l=l: stage_ffn(2 * l + 1, l * 3 + 2)))
    plan.append((False, stage_final))
    sidx = 0
    for n_done, (uses_w, fn) in enumerate(plan):
        if STOP_AFTER is not None and n_done >= STOP_AFTER:
            break
        if uses_w:
            convert_ahead(sidx + 2)
            sidx += 1
        fn()
    K.barrier()
    es0.close()
    return nc


def _blocks(W, ncol_blocks, cb):
    kc = W.shape[0] // 128
    return np.ascontiguousarray(W.reshape(kc, 128, ncol_blocks, cb).transpose(2, 1, 0, 3)).reshape(ncol_blocks * 128, kc * cb)


def _prep_shared(inp):
    sh = {}
    wgu = inp["ffn_w_gu"].reshape(8, D, 2 * DFF)
    out = np.empty((8, NGU * 128, 8192), np.float32)
    for f in range(8):
        W3 = wgu[f].reshape(KC, 128, 2 * DFF)
        g = W3[:, :, :DFF].reshape(KC, 128, NGU, 256)
        u = W3[:, :, DFF:].reshape(KC, 128, NGU, 256)
        gu = np.concatenate([g, u], axis=3)
        out[f] = gu.transpose(2, 1, 0, 3).reshape(NGU * 128, 8192)
    sh["wgu"] = out.reshape(8 * NGU * 128, 8192)
    wd = inp["ffn_w_down"].reshape(8, DFF, D)
    sh["wd"] = np.concatenate([_blocks(wd[f], 16, 128) for f in range(8)], axis=0)
    sh["wqkv"] = np.concatenate([_blocks(inp["attn_w_qkv"][j], 12, 512) for j in range(2)], axis=0)
    sh["wo"] = np.concatenate([_blocks(inp["attn_w_o"][j], 4, 512) for j in range(2)], axis=0)
    wins = []
    for j in range(2):
        W = inp["conv_w_in"][j]
        W3 = W.reshape(KC, 128, 3 * D)
        b = W3[:, :, :D].reshape(KC, 128, 4, 512)
        c = W3[:, :, D:2 * D].reshape(KC, 128, 8, 256)
        x = W3[:, :, 2 * D:].reshape(KC, 128, 8, 256)
        cx = np.concatenate([c, x], axis=3)
        allb = np.concatenate([b, cx], axis=2)
        wins.append(allb.transpose(2, 1, 0, 3).reshape(12 * 128, 8192))
    sh["win"] = np.concatenate(wins, axis=0)
    sh["wout"] = np.concatenate([_blocks(inp["conv_w_out"][j], 4, 512) for j in range(2)], axis=0)
    g_all = np.concatenate([inp["norm_g"].reshape(12, D), inp["final_norm_g"].reshape(1, D)], axis=0)
    sh["ng"] = np.ascontiguousarray(g_all.reshape(13, KC, 128).transpose(2, 0, 1)).reshape(128, 13 * 16)
    lq, lk = inp["attn_lambda_q"], inp["attn_lambda_k"]
    lam = np.stack([lq, lk], axis=-1)
    sh["lam"] = np.ascontiguousarray(lam.transpose(2, 0, 1, 3)).reshape(128, 8)
    sh["subg"] = np.ascontiguousarray(inp["attn_subln_g"].reshape(2, 2, 128).transpose(2, 0, 1)).reshape(128, 4)
    sh["convw"] = np.ascontiguousarray(inp["conv_w"].reshape(2, 3, KC, 128).transpose(3, 0, 1, 2)).reshape(128, 96)
    return {k: np.ascontiguousarray(v, dtype=np.float32) for k, v in sh.items()}


def _mask_for(j):
    k = np.arange(512)[:, None] // 64
    q = np.arange(512)[None, :] // 64
    diag = (k <= q).astype(np.float32)
    M = np.zeros((4, 512, 512), np.float32)
    for r in range(4):
        if r < j:
            M[r] = 1.0
        elif r == j:
            M[r] = diag
    return np.ascontiguousarray(M.reshape(4, 4, 128, 512).transpose(2, 0, 1, 3)).reshape(128, 16 * 512)


def kernel(**inp):
    inp = {k: np.asarray(v) for k, v in inp.items()}
    sh = _prep_shared(inp)
    xp, xsm = inp["x_prompt"], inp["x_sample"]
    in_maps = []
    for c in range(8):
        b, j = c // 4, c % 4
        cols = [xp[b, TS * (4 * i + j):TS * (4 * i + j + 1), :].T for i in range(NT)] + [xsm[c].T]
        m = dict(sh)
        m["xin"] = np.ascontiguousarray(np.concatenate(cols, axis=1), dtype=np.float32)
        ck = np.stack([inp["cache_k_l0"][c], inp["cache_k_l2"][c]], axis=0)
        m["cachek"] = np.ascontiguousarray(ck.transpose(0, 2, 3, 1)).reshape(2 * 16 * 128, PAST)
        cv = np.stack([inp["cache_v_l0"][c], inp["cache_v_l2"][c]], axis=0)
        m["cachev"] = np.ascontiguousarray(cv).reshape(2 * PAST, D)
        cs = np.stack([inp["state_conv_l1"][c], inp["state_conv_l3"][c]], axis=0)
        m["cstate"] = np.ascontiguousarray(cs.transpose(0, 2, 1)).reshape(2 * D, 2)
        m["mask"] = _mask_for(j)
        sel = np.zeros((128, 8), np.float32)
        if j > 0:
            sel[:, j - 1] = 1.0
        else:
            sel[:, 4 + 3] = 1.0
        m["sel"] = sel
        in_maps.append(m)

    nc = build_program()
    res = run_bass_kernel_spmd(nc, in_maps, core_ids=list(range(8)))
    R = res.results

    B, S = xp.shape[0], xp.shape[1]
    y_p = np.empty((B, S, D), np.float32)
    y_s = np.empty((8, NSMP, D), np.float32)
    KV = ("k0", "v0", "k2", "v2")
    kvp = {k: np.empty((B, S, D), np.float32) for k in KV}
    kvs = {k: np.empty((8, NSMP, D), np.float32) for k in ("k0", "v0", "k2", "v2")}
    cp = {k: np.empty((B, 2, D), np.float32) for k in ("c1p", "c3p")}
    csm = {k: np.empty((8, 2, D), np.float32) for k in ("c1s", "c3s")}
    for c in range(8):
        b, j = c // 4, c % 4
        r = dict(R[c])
        kvo = np.asarray(r["kvo"]).reshape(4, TOK, D)
        cvo = np.asarray(r["cvo"]).reshape(4, D, 2)
        for q_, k_ in enumerate(KV):
            r[k_] = kvo[q_]
        for q_, k_ in enumerate(("c1p", "c3p", "c1s", "c3s")):
            r[k_] = cvo[q_]
        yT = np.asarray(r["yT"])
        for i in range(NT):
            g0 = TS * (4 * i + j)
            y_p[b, g0:g0 + TS] = yT[:, i * TS:(i + 1) * TS].T
            for k in kvp:
                kvp[k][b, g0:g0 + TS] = np.asarray(r[k])[i * TS:(i + 1) * TS]
        y_s[c] = yT[:, NPR:].T
        for k in kvs:
            kvs[k][c] = np.asarray(r[k])[NPR:]
        if j == 3:
            for k in cp:
                cp[k][b] = np.asarray(r[k]).T
        for k in csm:
            csm[k][c] = np.asarray(r[k]).T
    H16 = (16, 128)
    H8 = (8, 256)
    return (y_p, y_s,
            kvp["k0"].reshape(B, S, *H16), kvp["v0"].reshape(B, S, *H8), cp["c1p"],
            kvp["k2"].reshape(B, S, *H16), kvp["v2"].reshape(B, S, *H8), cp["c3p"],
            kvs["k0"].reshape(8, NSMP, *H16), kvs["v0"].reshape(8, NSMP, *H8), csm["c1s"],
            kvs["k2"].reshape(8, NSMP, *H16), kvs["v2"].reshape(8, NSMP, *H8), csm["c3s"])
```
